# Optimizing a Trainium2 kernel written in Bass

```python
import math
import jax, jax.numpy as jnp
from jax import lax
import numpy as np

D_MODEL = 1024
BATCH = 2
SEQ = 8192
DEPTH = 2

HG_HEADS = 4
HG_DK = 128
HG_DV = 128
HG_WIDTH = HG_HEADS * HG_DK
HG_CHUNK = 64
DA_HEADS = 4
DA_DQK = 64
DA_DV = 2 * DA_DQK
DA_WIDTH = DA_HEADS * DA_DV
DA_QBLOCK = 128
ALIBI_MAX_BIAS = 8.0
D_FF = ((8 * D_MODEL // 3 + 255) // 256) * 256
N_BRANCH = 2
EPS = 1e-6
IN_SIZES = (HG_HEADS * HG_DK, HG_HEADS * HG_DK, HG_HEADS * HG_DV, HG_HEADS * HG_DV,
            DA_HEADS * 2 * DA_DQK, DA_HEADS * 2 * DA_DQK, DA_HEADS * DA_DV,
            N_BRANCH * D_MODEL)
D_IN = sum(IN_SIZES)

kernel_name = "hybrid_hgrn2_diffattn_gated_block"


def _split_points():
    return tuple(int(v) for v in np.cumsum(np.array(IN_SIZES))[:-1])


def rmsnorm(x, gain):
    xf = x.astype(jnp.float32)
    y = xf * lax.rsqrt(jnp.mean(xf * xf, axis=-1, keepdims=True) + EPS)
    return (y * gain.astype(jnp.float32)).astype(x.dtype)


def hgrn2_mix(q_raw, f_raw, i_raw, g_raw, lb, out_gain):
    B, S, _ = q_raw.shape
    f32 = jnp.float32
    lb = lb.astype(f32)
    z = f_raw.astype(f32)
    log_f = jnp.logaddexp(jnp.log(lb), jnp.log1p(-lb) + jax.nn.log_sigmoid(z))
    k = (1.0 - lb) * jax.nn.sigmoid(-z)
    q = jax.nn.silu(q_raw.astype(f32))
    v = i_raw.astype(f32)
    n = S // HG_CHUNK

    def to_chunks(t, d):
        return t.reshape(B, n, HG_CHUNK, HG_HEADS, d).transpose(1, 0, 3, 2, 4)

    causal = jnp.tril(jnp.ones((HG_CHUNK, HG_CHUNK), dtype=bool))

    def step(state, xs):
        qc, kc, vc, lfc = xs
        b = jnp.cumsum(lfc, axis=-2)
        rel = b[..., :, None, :] - b[..., None, :, :]
        decay = jnp.exp(jnp.where(causal[:, :, None], rel, -jnp.inf))
        scores = jnp.einsum('bhtd,bhtsd,bhsd->bhts', qc, decay, kc)
        o = jnp.einsum('bhts,bhse->bhte', scores, vc) + jnp.einsum('bhtd,bhde->bhte', qc * jnp.exp(b), state)
        b_last = b[..., -1:, :]
        new_state = jnp.exp(b_last[..., 0, :])[..., None] * state + jnp.einsum(
            'bhsd,bhse->bhde', kc * jnp.exp(b_last - b), vc)
        return new_state, o

    s0 = jnp.zeros((B, HG_HEADS, HG_DK, HG_DV), f32)
    _, o = lax.scan(step, s0, (to_chunks(q, HG_DK), to_chunks(k, HG_DK),
                               to_chunks(v, HG_DV), to_chunks(log_f, HG_DK)))
    o = o.transpose(1, 0, 3, 2, 4).reshape(B, S, HG_HEADS, HG_DV)
    g = jax.nn.silu(g_raw.astype(f32)).reshape(B, S, HG_HEADS, HG_DV)
    o = rmsnorm(o, out_gain) * g
    return o.reshape(B, S, HG_HEADS * HG_DV).astype(q_raw.dtype)


def diff_attention(q_raw, k_raw, v_raw, lam, lam_init, subln_gain):
    B, S, _ = q_raw.shape
    f32 = jnp.float32
    q = q_raw.reshape(B, S, DA_HEADS, 2, DA_DQK).transpose(0, 2, 3, 1, 4)
    k = k_raw.reshape(B, S, DA_HEADS, 2, DA_DQK).transpose(0, 2, 3, 1, 4)
    v = v_raw.reshape(B, S, DA_HEADS, DA_DV).transpose(0, 2, 1, 3)
    nb = S // DA_QBLOCK
    qb = q.reshape(B, DA_HEADS, 2, nb, DA_QBLOCK, DA_DQK).transpose(3, 0, 1, 2, 4, 5)
    slopes = jnp.exp2(-ALIBI_MAX_BIAS / DA_HEADS * jnp.arange(1, DA_HEADS + 1, dtype=f32))
    key_pos = jnp.arange(S)
    scale = 1.0 / math.sqrt(DA_DQK)

    def block(args):
        qblk, idx = args
        s = jnp.einsum('bhcqd,bhckd->bhcqk', qblk, k).astype(f32) * scale
        dist = (idx * DA_QBLOCK + jnp.arange(DA_QBLOCK))[:, None] - key_pos[None, :]
        alibi = slopes[:, None, None] * dist.astype(f32)[None]
        s = jnp.where(dist >= 0, s - alibi[None, :, None], -jnp.inf)
        p = jax.nn.softmax(s, axis=-1)
        p = p[:, :, 0] - lam * p[:, :, 1]
        return jnp.einsum('bhqk,bhke->bhqe', p.astype(v.dtype), v)

    o = lax.map(block, (qb, jnp.arange(nb)))
    o = o.transpose(1, 0, 3, 2, 4).reshape(B, S, DA_HEADS, DA_DV)
    o = rmsnorm(o, subln_gain) * (1.0 - lam_init)
    return o.reshape(B, S, DA_WIDTH).astype(q_raw.dtype)


def setup_inputs(seed: int = 0) -> dict:
    key = jax.random.key(seed)
    ks = jax.random.split(key, 20)
    nrm = jax.random.normal

    def w(k, shape, fan_in):
        return nrm(k, shape, jnp.float32) * fan_in ** -0.5

    def gain(k, shape):
        return 1.0 + 0.02 * nrm(k, shape, jnp.float32)

    return {
        "x": nrm(ks[0], (BATCH, SEQ, D_MODEL), jnp.float32),
        "lower_bounds": 0.1 * nrm(ks[1], (DEPTH, HG_WIDTH), jnp.float32),
        "norm_mix_pre": gain(ks[2], (DEPTH, D_MODEL)),
        "norm_mix_post": gain(ks[3], (DEPTH, D_MODEL)),
        "norm_ffn_pre": gain(ks[4], (DEPTH, D_MODEL)),
        "norm_ffn_post": gain(ks[5], (DEPTH, D_MODEL)),
        "w_in": w(ks[6], (DEPTH, D_MODEL, D_IN), D_MODEL),
        "hg_out_norm": gain(ks[7], (DEPTH, HG_DV)),
        "da_subln": gain(ks[8], (DEPTH, DA_DV)),
        "lambda_q1": 0.1 * nrm(ks[9], (DEPTH, DA_DQK), jnp.float32),
        "lambda_k1": 0.1 * nrm(ks[10], (DEPTH, DA_DQK), jnp.float32),
        "lambda_q2": 0.1 * nrm(ks[11], (DEPTH, DA_DQK), jnp.float32),
        "lambda_k2": 0.1 * nrm(ks[12], (DEPTH, DA_DQK), jnp.float32),
        "w_up_a": w(ks[13], (DEPTH, HG_WIDTH, D_MODEL), HG_WIDTH),
        "w_up_b": w(ks[14], (DEPTH, DA_WIDTH, D_MODEL), DA_WIDTH),
        "w_out": w(ks[15], (DEPTH, D_MODEL, D_MODEL), D_MODEL),
        "w_ffn_gate": w(ks[16], (DEPTH, D_MODEL, D_FF), D_MODEL),
        "w_ffn_up": w(ks[17], (DEPTH, D_MODEL, D_FF), D_MODEL),
        "w_ffn_down": w(ks[18], (DEPTH, D_FF, D_MODEL), D_FF),
    }


def reference(x, lower_bounds, norm_mix_pre, norm_mix_post, norm_ffn_pre, norm_ffn_post,
              w_in, hg_out_norm, da_subln, lambda_q1, lambda_k1, lambda_q2, lambda_k2,
              w_up_a, w_up_b, w_out, w_ffn_gate, w_ffn_up, w_ffn_down):
    f32 = jnp.float32
    lb_all = jnp.cumsum(jax.nn.softmax(lower_bounds.astype(f32), axis=0), axis=0)
    lb_all = lb_all - lb_all[0:1]
    split_pts = _split_points()
    for l in range(DEPTH):
        h = rmsnorm(x, norm_mix_pre[l])
        proj = jnp.einsum('bsd,de->bse', h, w_in[l])
        q_a, f_a, i_a, g_a, q_b, k_b, v_b, gate_raw = jnp.split(proj, split_pts, axis=-1)
        y_a = hgrn2_mix(q_a, f_a, i_a, g_a, lb_all[l], hg_out_norm[l])
        lam_init = 0.8 - 0.6 * math.exp(-0.3 * l)
        lam = (jnp.exp(jnp.sum(lambda_q1[l].astype(f32) * lambda_k1[l].astype(f32)))
               - jnp.exp(jnp.sum(lambda_q2[l].astype(f32) * lambda_k2[l].astype(f32))) + lam_init)
        y_b = diff_attention(q_b, k_b, v_b, lam, lam_init, da_subln[l])
        gate_a, gate_b = jnp.split(jax.nn.sigmoid(gate_raw), N_BRANCH, axis=-1)
        merged = (gate_a * jnp.einsum('bsc,cd->bsd', y_a, w_up_a[l])
                  + gate_b * jnp.einsum('bsc,cd->bsd', y_b, w_up_b[l]))
        mix = jnp.einsum('bsd,de->bse', merged, w_out[l])
        x = x + rmsnorm(mix, norm_mix_post[l])
        h = rmsnorm(x, norm_ffn_pre[l])
        ff = jax.nn.silu(jnp.einsum('bsd,df->bsf', h, w_ffn_gate[l])) * jnp.einsum('bsd,df->bsf', h, w_ffn_up[l])
        ff = jnp.einsum('bsf,fd->bsd', ff, w_ffn_down[l])
        x = x + rmsnorm(ff, norm_ffn_post[l])
    return x
```

```python
import contextlib
import math
import numpy as np
import ml_dtypes
import concourse.bass as bass
import concourse.mybir as mybir
from concourse.bass_utils import run_bass_kernel_spmd

F32 = mybir.dt.float32
BF16 = mybir.dt.bfloat16
AF = mybir.ActivationFunctionType
ALU = mybir.AluOpType

D = 1024
S = 8192
NB = 2
DEPTH = 2
DFF = 2816
NFC = DFF // 128
EPS = 1e-6
TS = 2048
NT = S // 512


class Sem:
    def __init__(self, h):
        self.h = h
        self.n = 0


class Buf:
    def __init__(self, ap=None):
        self.ap = ap
        self.w = {}
        self.r = {}


def _merge(d, t):
    for k, v in t.items():
        if d.get(k, 0) < v:
            d[k] = v


class Eng:
    def __init__(self, K, name, e):
        self.K = K
        self.name = name
        self.e = e
        self.sem = K.new_sem("p_" + name)
        self.waited = {}

    def wait(self, toks):
        for sem, v in toks.items():
            if sem is self.sem and self.name == "pe":
                continue
            if self.waited.get(sem, 0) >= v:
                continue
            self.e.wait_ge(sem.h, v)
            self.waited[sem] = v

    def mark(self, ins):
        self.sem.n += 1
        ins.then_inc(self.sem.h, 1)
        return {self.sem: self.sem.n}


class Kctx:
    def __init__(self, nc, st):
        self.nc = nc
        self.st = st
        self.nsem = 0
        self.pe = Eng(self, "pe", nc.tensor)
        self.act = Eng(self, "act", nc.scalar)
        self.dve = Eng(self, "dve", nc.vector)
        self.pool = Eng(self, "pool", nc.gpsimd)
        self.sp = Eng(self, "sp", nc.sync)
        self.engs = [self.pe, self.act, self.dve, self.pool, self.sp]

    def new_sem(self, name):
        self.nsem += 1
        return Sem(self.st.enter_context(self.nc.semaphore(name)))

    def sb(self, name, shape, dt):
        return self.st.enter_context(self.nc.sbuf_tensor(name, shape, dt))

    def ps(self, name, shape, dt):
        return self.st.enter_context(self.nc.psum_tensor(name, shape, dt))

    def begin(self, eng, reads=(), writes=()):
        for b in reads:
            eng.wait(b.w)
        for b in writes:
            eng.wait(b.w)
            eng.wait(b.r)

    def end(self, eng, ins, reads=(), writes=()):
        tok = eng.mark(ins)
        for b in reads:
            _merge(b.r, tok)
        for b in writes:
            b.w = dict(tok)
            b.r = {}
        return tok

    def op(self, eng, fn, reads=(), writes=()):
        self.begin(eng, reads, writes)
        ins = fn()
        return self.end(eng, ins, reads, writes)

    def dma(self, q, sem, out, in_, reads=(), writes=(), **kw):
        self.begin(q, reads, writes)
        ins = q.e.dma_start(out=out, in_=in_, **kw)
        sem.n += 16
        ins.then_inc(sem.h, 16)
        tok = {sem: sem.n}
        for b in reads:
            _merge(b.r, tok)
        for b in writes:
            b.w = dict(tok)
            b.r = {}
        return tok

    def barrier(self):
        for e in self.engs:
            for o in self.engs:
                if o is not e and o.sem.n > 0:
                    e.wait({o.sem: o.sem.n})


class Consts:
    pass


def make_consts(K):
    nc = K.nc
    C = Consts()
    C.ident = K.sb("ident", [128, 128], BF16)
    C.b_ident = Buf()
    K.op(K.pool, lambda: nc.gpsimd.memset(C.ident[:], 1.0), writes=[C.b_ident])
    K.op(K.pool, lambda: nc.gpsimd.affine_select(out=C.ident[:], in_=C.ident[:], pattern=[[1, 128]],
                                                 compare_op=ALU.is_equal, fill=0.0, base=0, channel_multiplier=-1),
         writes=[C.b_ident])
    return C


def rstd_from_ss(K, ss, lnv, rstd, n, b_ss, b_rstd, post_scale=None):
    nc = K.nc
    K.op(K.act, lambda: nc.scalar.activation(out=lnv, in_=ss, func=AF.Ln, scale=1.0 / n, bias=K.eps_col[:, 0:1]),
         reads=[b_ss], writes=[b_rstd])
    K.op(K.act, lambda: nc.scalar.activation(out=rstd, in_=lnv, func=AF.Exp, scale=-0.5), reads=[], writes=[b_rstd])


def emit_AB(K, C, l, hT_tile, w_hd, lbh, rowv, alibi, yT, b_hT_src=None, b_y_dst=None, ntiles=NT, stop=0):
    nc = K.nc
    pe, act, dve, pool, sp = K.pe, K.act, K.dve, K.pool, K.sp
    lam_init = 0.8 - 0.6 * math.exp(-0.3 * l)
    with contextlib.ExitStack() as st:
        def sb(name, shape, dt):
            return st.enter_context(nc.sbuf_tensor(name + f"_{l}", shape, dt))

        def psum(name, shape, dt):
            return st.enter_context(nc.psum_tensor(name + f"_{l}", shape, dt))

        W = sb("W", [128, 8, 896], BF16)
        hb = [sb(f"hb{i}", [128, 8, 512], BF16) for i in range(2)]
        KT = [sb(f"KT{c}", [68, S], BF16) for c in range(2)]
        QT = [sb(f"QT{c}", [68, S], BF16) for c in range(2)]
        Vb = sb("Vb", [128, S // 128, 132], BF16)
        rv = sb("rv", [128, 512], F32)
        lbt = sb("lbt", [128, 2], F32)
        cols = sb("cols", [128, 16], F32)
        gn_a = sb("gn_a", [128, 128], F32)
        gn_b = sb("gn_b", [128, 128], F32)
        junk = sb("junk", [128, 512], F32)
        hgmask = sb("hgmask", [128, 128], BF16)
        trimask = sb("trimask", [128, 2, 128], BF16)
        scanm = sb("scanm", [128, 512], F32)
        sig = sb("sig", [128, 512], F32)
        lf = sb("lf", [128, 512], F32)
        bb = sb("bb", [128, 512], F32)
        d1 = sb("d1", [128, 512], F32)
        d3 = sb("d3", [128, 512], F32)
        e1 = sb("e1", [128, 512], F32)
        e2 = sb("e2", [128, 512], F32)
        e3 = sb("e3", [128, 512], F32)
        kk = sb("kk", [128, 512], F32)
        qs = sb("qs", [128, 512], F32)
        eL = sb("eL", [128, 8], F32)
        er = sb("er", [128, 8], F32)
        qtT = sb("qtT", [128, 512], BF16)
        ktT = sb("ktT", [128, 512], BF16)
        kdT = sb("kdT", [128, 512], BF16)
        kd = sb("kd", [128, 4, 128], BF16)
        va = sb("va", [128, 4, 128], BF16)
        sg = sb("sg", [128, 4, 128], BF16)
        Sst = sb("Sst", [128, 128], F32)
        Sb16 = [sb(f"Sb16_{i}", [128, 128], BF16) for i in range(2)]
        scsb = sb("scsb", [128, 128], BF16)
        ssc = sb("ssc", [128, 8], F32)
        ot = sb("ot", [128, 128], F32)
        ya = sb("ya", [128, 4, 128], BF16)
        yst = [[sb(f"yst{a}{i}", [128, 512], BF16) for i in range(2)] for a in range(2)]
        PT = [sb(f"PT{i}", [128, 512], BF16) for i in range(3)]
        fin = sb("fin", [128, 8], F32)
        t1 = sb("t1", [128, 128], F32)
        ob = sb("ob", [128, 128], F32)
        yb = sb("yb", [128, 4, 128], BF16)
        pbP = [psum(f"pbP{i}", [128, 512], F32) for i in range(2)]
        pbM = psum("pbM", [128, 512], F32)
        pbO = psum("pbO", [128, 512], F32)
        pbS = [psum(f"pbS{i}", [128, 512], F32) for i in range(2)]
        pbA = [psum(f"pbA{i}", [128, 512], F32) for i in range(2)]
        pbT = pbM[:, 256:512].bitcast(BF16)

        B = lambda: Buf()
        b_W, b_rv, b_lbt, b_cols, b_gn, b_masks = B(), B(), B(), B(), B(), B()
        b_hb = [B(), B()]
        b_KT = [B() for _ in range(ntiles)]
        b_QT = [B() for _ in range(ntiles)]
        b_Vb = [B() for _ in range(ntiles)]
        b_alibi = B()
        b_pbP = [B(), B()]
        b_sig, b_lf, b_bb, b_d1, b_d3, b_e1, b_e2, b_e3, b_kk, b_qs, b_eLr = [B() for _ in range(11)]
        b_qtT, b_ktT, b_kdT, b_kd, b_va, b_sg = [B() for _ in range(6)]
        b_S, b_Sb = B(), [B(), B()]
        b_pbM = B()
        b_pbM_sc = b_pbM_s = b_pbM
        b_pbO, b_scsb, b_ssc, b_ot, b_ya = B(), B(), B(), B(), B()
        b_yst = [[B(), B()], [B(), B()]]
        b_PT = [B(), B(), B()]
        b_pbS = [B(), B()]
        b_pbA = B()
        b_fin, b_t1, b_ob, b_yb = B(), B(), B(), B()
        b_junk = B()
        s_hb = [K.new_sem(f"ab_hb{l}_{i}") for i in range(2)]
        s_y = [[K.new_sem(f"ab_y{l}_{a}{i}") for i in range(2)] for a in range(2)]

        wv = w_hd.rearrange("(c p) n -> p c n", p=128)
        s_W = K.new_sem(f"ab_W{l}")
        for c4 in range(4):
            K.dma(pool, s_W, W[:, 2 * c4:2 * c4 + 2, :], wv[:, 2 * c4:2 * c4 + 2, :])
        b_W.w = {s_W: s_W.n}
        K.dma(sp, K.new_sem(f"ab_rv{l}"), rv[:], rowv.partition_broadcast(128), writes=[b_rv])
        K.dma(sp, K.new_sem(f"ab_lb{l}"), lbt[:], lbh, writes=[b_lbt])
        s_al = K.new_sem(f"ab_al{l}")
        for c in range(2):
            K.dma(pool, s_al, QT[c][64:68, :], alibi[0:4, :], max_dma_last_dim=4096)
            K.dma(pool, s_al, KT[c][64:68, :], alibi[4:8, :], max_dma_last_dim=4096)
        b_alibi.w = {s_al: s_al.n}
        K.op(pool, lambda: nc.gpsimd.memset(hgmask[:], 1.0), writes=[b_masks])
        K.op(pool, lambda: nc.gpsimd.affine_select(out=hgmask[:], in_=hgmask[:], pattern=[[1, 128]],
                                                   compare_op=ALU.is_ge, fill=0.0, base=0, channel_multiplier=-1),
             writes=[b_masks])
        for c in range(2):
            K.op(pool, lambda c=c: nc.gpsimd.tensor_copy(out=trimask[:, c, :], in_=hgmask[:]), writes=[b_masks])
        K.op(pool, lambda: nc.gpsimd.memset(hgmask[0:64, 64:128], 0.0), writes=[b_masks])
        K.op(pool, lambda: nc.gpsimd.memset(scanm[:], 1.0), writes=[b_masks])
        K.op(pool, lambda: nc.gpsimd.memset(scanm[:].rearrange("p (c k) -> p c k", k=64)[:, :, 0:1], 0.0),
             writes=[b_masks])
        K.op(pool, lambda: nc.gpsimd.memset(Vb[:, :, 128:129], 1.0), writes=[b_alibi])
        K.op(pool, lambda: nc.gpsimd.memset(Sst[:], 0.0), writes=[b_S])
        if l == 0:
            K.op(dve, lambda: nc.vector.memset(cols[:, 0:1], 0.0), reads=[b_lbt], writes=[b_cols])
        else:
            K.op(dve, lambda: nc.vector.tensor_tensor(out=cols[:, 7:8], in0=lbt[:, 1:2], in1=lbt[:, 0:1], op=ALU.subtract),
                 reads=[b_lbt], writes=[b_cols])
            K.op(act, lambda: nc.scalar.activation(out=cols[:, 0:1], in_=cols[:, 7:8], func=AF.Sigmoid),
                 reads=[b_cols], writes=[b_cols])
        K.op(dve, lambda: nc.vector.tensor_scalar(out=cols[:, 1:2], in0=cols[:, 0:1], scalar1=-1.0, scalar2=1.0,
                                                  op0=ALU.mult, op1=ALU.add), reads=[b_cols], writes=[b_cols])
        K.op(dve, lambda: nc.vector.tensor_scalar(out=cols[:, 2:3], in0=cols[:, 1:2], scalar1=-1.0, scalar2=None,
                                                  op0=ALU.mult), reads=[b_cols], writes=[b_cols])
        K.op(dve, lambda: nc.vector.scalar_tensor_tensor(out=junk[:, 0:64], in0=rv[:, 256:320], scalar=1.0, in1=rv[:, 320:384],
                                                         op0=ALU.mult, op1=ALU.mult, accum_out=cols[:, 3:4]),
             reads=[b_rv], writes=[b_cols, b_junk])
        K.op(dve, lambda: nc.vector.scalar_tensor_tensor(out=junk[:, 0:64], in0=rv[:, 384:448], scalar=1.0, in1=rv[:, 448:512],
                                                         op0=ALU.mult, op1=ALU.mult, accum_out=cols[:, 4:5]),
             reads=[b_rv], writes=[b_cols, b_junk])
        K.op(act, lambda: nc.scalar.activation(out=cols[:, 3:5], in_=cols[:, 3:5], func=AF.Exp), reads=[b_cols], writes=[b_cols])
        K.op(dve, lambda: nc.vector.tensor_tensor(out=cols[:, 5:6], in0=cols[:, 3:4], in1=cols[:, 4:5], op=ALU.subtract),
             reads=[b_cols], writes=[b_cols])
        K.op(dve, lambda: nc.vector.tensor_scalar(out=cols[:, 5:6], in0=cols[:, 5:6], scalar1=lam_init, scalar2=None, op0=ALU.add),
             reads=[b_cols], writes=[b_cols])
        K.op(dve, lambda: nc.vector.tensor_scalar(out=cols[:, 6:7], in0=cols[:, 5:6], scalar1=-1.0, scalar2=None, op0=ALU.mult),
             reads=[b_cols], writes=[b_cols])
        K.op(dve, lambda: nc.vector.tensor_copy(out=gn_a[:], in_=rv[:, 0:128]), reads=[b_rv], writes=[b_gn])
        K.op(dve, lambda: nc.vector.tensor_scalar(out=gn_b[:], in0=rv[:, 128:256], scalar1=1.0 - lam_init, scalar2=None,
                                                  op0=ALU.mult), reads=[b_rv], writes=[b_gn])

        def load_h(i):
            K.dma(sp, s_hb[i % 2], hb[i % 2][:], hT_tile(i), reads=([b_hT_src] if b_hT_src else []), writes=[b_hb[i % 2]])

        if stop == 1:
            K.barrier()
            return
        load_h(0)
        pidx = [0]
        ptidx = [0]

        def transpose4(dst_ap, src_fn, src_buf, dst_buf):
            K.begin(pe, reads=[src_buf, C.b_ident], writes=[b_pbM])
            for q in range(4):
                ins = nc.tensor.transpose(pbT[:, q * 128:(q + 1) * 128], src_fn(q), C.ident[:])
            K.end(pe, ins, reads=[src_buf, C.b_ident], writes=[b_pbM])
            K.op(dve, lambda: nc.vector.tensor_copy(out=dst_ap, in_=pbT[:, 0:512]), reads=[b_pbM], writes=[dst_buf])

        for i in range(ntiles):
            h = hb[i % 2]
            bh = b_hb[i % 2]
            if i + 1 < ntiles:
                load_h(i + 1)
            tok0 = i * 512

            def proj_fm(col0, M):
                p = pidx[0] % 2
                pidx[0] += 1
                K.begin(pe, reads=[bh, b_W], writes=[b_pbP[p]])
                for dc in range(8):
                    ins = nc.tensor.matmul(pbP[p][0:M, :], lhsT=W[:, dc, col0:col0 + M], rhs=h[:, dc, :],
                                           start=(dc == 0), stop=(dc == 7))
                K.end(pe, ins, reads=[bh, b_W], writes=[b_pbP[p]])
                return p

            p = proj_fm(0, 128)
            K.op(act, lambda: nc.scalar.activation(out=junk[:], in_=pbP[p][:, :], func=AF.Sigmoid), reads=[b_pbP[p]], writes=[b_junk])
            K.op(dve, lambda: nc.vector.tensor_tensor(out=qs[:], in0=pbP[p][:, :], in1=junk[:], op=ALU.mult), reads=[b_pbP[p], b_junk], writes=[b_qs])
            p = proj_fm(128, 128)
            K.op(act, lambda: nc.scalar.activation(out=sig[:], in_=pbP[p][:, :], func=AF.Sigmoid), reads=[b_pbP[p]], writes=[b_sig])
            for c in range(2):
                p = proj_fm(256 + c * 128, 64)
                K.op(dve, lambda: nc.vector.tensor_scalar(out=QT[c][0:64, tok0:tok0 + 512], in0=pbP[p][0:64, :], scalar1=0.125,
                                                          scalar2=None, op0=ALU.mult), reads=[b_pbP[p]], writes=[b_QT[i]])
                p = proj_fm(256 + c * 128 + 64, 64)
                K.op(dve, lambda: nc.vector.tensor_copy(out=KT[c][0:64, tok0:tok0 + 512], in_=pbP[p][0:64, :]),
                     reads=[b_pbP[p]], writes=[b_KT[i]])
            if stop == 23:
                K.barrier()
                return
            for sbt in range(4):
                p = pidx[0] % 2
                pidx[0] += 1
                K.begin(pe, reads=[bh, b_W], writes=[b_pbP[p]])
                for dc in range(8):
                    ins = nc.tensor.matmul(pbP[p][:, 0:384], lhsT=h[:, dc, sbt * 128:(sbt + 1) * 128], rhs=W[:, dc, 512:896],
                                           start=(dc == 0), stop=(dc == 7))
                K.end(pe, ins, reads=[bh, b_W], writes=[b_pbP[p]])
                K.op(act, lambda: nc.scalar.activation(out=junk[:, 0:128], in_=pbP[p][:, 128:256], func=AF.Sigmoid),
                     reads=[b_pbP[p]], writes=[b_junk])
                K.op(dve, lambda: nc.vector.tensor_tensor(out=sg[:, sbt, :], in0=pbP[p][:, 128:256], in1=junk[:, 0:128], op=ALU.mult),
                     reads=[b_pbP[p], b_junk], writes=[b_sg])
                K.op(dve, lambda: nc.vector.tensor_copy(out=va[:, sbt, :], in_=pbP[p][:, 0:128]), reads=[b_pbP[p]], writes=[b_va])
                K.op(dve, lambda: nc.vector.tensor_copy(out=Vb[:, i * 4 + sbt, 0:128], in_=pbP[p][:, 256:384]),
                     reads=[b_pbP[p], b_alibi], writes=[b_Vb[i]])

            if stop in (2, 231, 232, 233):
                K.barrier()
                return
            K.op(act, lambda: nc.scalar.activation(out=lf[:], in_=sig[:], func=AF.Ln, scale=cols[:, 1:2], bias=cols[:, 0:1]),
                 reads=[b_sig, b_cols], writes=[b_lf])
            K.op(dve, lambda: nc.vector.tensor_scalar(out=kk[:], in0=sig[:], scalar1=cols[:, 2:3], scalar2=cols[:, 1:2],
                                                      op0=ALU.mult, op1=ALU.add), reads=[b_sig, b_cols], writes=[b_kk])
            K.op(dve, lambda: nc.vector.tensor_tensor_scan(out=bb[:], data0=scanm[:], data1=lf[:], initial=0.0,
                                                           op0=ALU.mult, op1=ALU.add), reads=[b_lf, b_masks], writes=[b_bb])
            bb3 = bb[:].rearrange("p (c k) -> p c k", k=64)
            K.op(dve, lambda: nc.vector.tensor_tensor(out=d1[:].rearrange("p (c k) -> p c k", k=64), in0=bb3,
                                                      in1=bb3[:, :, 31:32].broadcast_to([128, 8, 64]), op=ALU.subtract),
                 reads=[b_bb], writes=[b_d1])
            K.op(dve, lambda: nc.vector.tensor_tensor(out=d3[:].rearrange("p (c k) -> p c k", k=64), in0=bb3,
                                                      in1=bb3[:, :, 63:64].broadcast_to([128, 8, 64]), op=ALU.subtract),
                 reads=[b_bb], writes=[b_d3])
            K.op(act, lambda: nc.scalar.activation(out=e1[:], in_=d1[:], func=AF.Exp), reads=[b_d1], writes=[b_e1])
            K.op(act, lambda: nc.scalar.activation(out=e2[:], in_=d1[:], func=AF.Exp, scale=-1.0), reads=[b_d1], writes=[b_e2])
            K.op(act, lambda: nc.scalar.activation(out=e3[:], in_=d3[:], func=AF.Exp, scale=-1.0), reads=[b_d3], writes=[b_e3])
            K.op(act, lambda: nc.scalar.activation(out=eL[:].rearrange("p (c o) -> p c o", o=1), in_=bb3[:, :, 63:64], func=AF.Exp),
                 reads=[b_bb], writes=[b_eLr])
            K.op(act, lambda: nc.scalar.activation(out=er[:].rearrange("p (c o) -> p c o", o=1), in_=bb3[:, :, 31:32], func=AF.Exp),
                 reads=[b_bb], writes=[b_eLr])
            K.op(dve, lambda: nc.vector.tensor_tensor(out=qtT[:], in0=qs[:], in1=e1[:], op=ALU.mult), reads=[b_qs, b_e1], writes=[b_qtT])
            K.op(dve, lambda: nc.vector.tensor_tensor(out=ktT[:], in0=kk[:], in1=e2[:], op=ALU.mult), reads=[b_kk, b_e2], writes=[b_ktT])
            K.op(dve, lambda: nc.vector.tensor_tensor(out=kdT[:], in0=kk[:], in1=e3[:], op=ALU.mult), reads=[b_kk, b_e3], writes=[b_kdT])
            transpose4(kd[:].rearrange("p a b -> p (a b)"), lambda q: kdT[:, q * 128:(q + 1) * 128], b_kdT, b_kd)

            if stop == 3:
                K.barrier()
                return
            for blk in range(4):
                cA, cB = 2 * blk, 2 * blk + 1
                c0 = blk * 128
                K.op(pe, lambda: nc.tensor.matmul(pbM[:, 0:128], lhsT=ktT[:, c0:c0 + 128], rhs=qtT[:, c0:c0 + 128], start=True, stop=True),
                     reads=[b_ktT, b_qtT], writes=[b_pbM_sc])
                K.op(dve, lambda: nc.vector.tensor_tensor(out=scsb[:], in0=pbM[:, 0:128], in1=hgmask[:], op=ALU.mult),
                     reads=[b_pbM_sc, b_masks], writes=[b_scsb])
                K.op(dve, lambda: nc.vector.tensor_scalar(out=Sb16[0][:], in0=Sst[:], scalar1=er[:, cA:cA + 1], scalar2=None, op0=ALU.mult),
                     reads=[b_S, b_eLr], writes=[b_Sb[0]])
                K.begin(pe, reads=[b_scsb, b_va, b_qtT, b_Sb[0]], writes=[b_pbO])
                nc.tensor.matmul(pbO[:, 0:128], lhsT=scsb[:], rhs=va[:, blk, :], start=True, stop=False)
                ins = nc.tensor.matmul(pbO[0:64, 0:128], lhsT=qtT[:, c0:c0 + 64], rhs=Sb16[0][:], start=False, stop=True)
                K.end(pe, ins, reads=[b_scsb, b_va, b_qtT, b_Sb[0]], writes=[b_pbO])
                K.op(pe, lambda: nc.tensor.matmul(pbM[:, 128:256], lhsT=kd[0:64, blk, :], rhs=va[0:64, blk, :], start=True, stop=True),
                     reads=[b_kd, b_va], writes=[b_pbM_s])
                K.op(dve, lambda: nc.vector.scalar_tensor_tensor(out=Sst[:], in0=Sst[:], scalar=eL[:, cA:cA + 1], in1=pbM[:, 128:256],
                                                                 op0=ALU.mult, op1=ALU.add), reads=[b_pbM_s, b_eLr], writes=[b_S])
                K.op(dve, lambda: nc.vector.tensor_scalar(out=Sb16[1][:], in0=Sst[:], scalar1=er[:, cB:cB + 1], scalar2=None, op0=ALU.mult),
                     reads=[b_S, b_eLr], writes=[b_Sb[1]])
                K.op(pe, lambda: nc.tensor.matmul(pbO[64:128, 0:128], lhsT=qtT[:, c0 + 64:c0 + 128], rhs=Sb16[1][:], start=False, stop=True),
                     reads=[b_qtT, b_Sb[1]], writes=[b_pbO])
                K.op(pe, lambda: nc.tensor.matmul(pbM[:, 128:256], lhsT=kd[64:128, blk, :], rhs=va[64:128, blk, :], start=True, stop=True),
                     reads=[b_kd, b_va], writes=[b_pbM_s])
                K.op(dve, lambda: nc.vector.scalar_tensor_tensor(out=Sst[:], in0=Sst[:], scalar=eL[:, cB:cB + 1], in1=pbM[:, 128:256],
                                                                 op0=ALU.mult, op1=ALU.add), reads=[b_pbM_s, b_eLr], writes=[b_S])
                K.op(act, lambda: nc.scalar.activation(out=junk[:, 0:128], in_=pbO[:, 0:128], func=AF.Square, accum_out=ssc[:, 0:1]),
                     reads=[b_pbO], writes=[b_ssc, b_junk])
                K.op(act, lambda: nc.scalar.activation(out=ssc[:, 1:2], in_=ssc[:, 0:1], func=AF.Ln, scale=1.0 / 128, bias=K.eps_col[:, 0:1]),
                     reads=[K.b_eps], writes=[b_ssc])
                K.op(act, lambda: nc.scalar.activation(out=ssc[:, 2:3], in_=ssc[:, 1:2], func=AF.Exp, scale=-0.5), writes=[b_ssc])
                K.op(dve, lambda: nc.vector.scalar_tensor_tensor(out=ot[:], in0=pbO[:, 0:128], scalar=ssc[:, 2:3], in1=gn_a[:],
                                                                 op0=ALU.mult, op1=ALU.mult), reads=[b_pbO, b_ssc, b_gn], writes=[b_ot])
                K.op(pool, lambda: nc.gpsimd.tensor_tensor(out=ya[:, blk, :], in0=ot[:], in1=sg[:, blk, :], op=ALU.mult),
                     reads=[b_ot, b_sg], writes=[b_ya])
            transpose4(yst[0][i % 2][:], lambda q: ya[:, q, :], b_ya, b_yst[0][i % 2])
            K.dma(sp, s_y[0][i % 2], yT[0, :, tok0:tok0 + 512], yst[0][i % 2][:], reads=[b_yst[0][i % 2]],
                  writes=([b_y_dst] if b_y_dst else []))

            if stop == 4:
                K.barrier()
                return
            for qt in range(2):
                i0 = tok0 + qt * 256
                nk = i0 // 128 + 2
                first = [True, True]
                for kt in range(nk):
                    j0 = kt * 128
                    full = j0 + 128 <= i0
                    diag0 = (j0 == i0)
                    diag1 = (j0 == i0 + 128)
                    q_lo = 128 if diag1 else 0
                    nq = 256 - q_lo
                    sI = ptidx[0] % 2
                    pI = ptidx[0] % 3
                    ptidx[0] += 1
                    kbuf = b_KT[j0 // 512]
                    K.begin(pe, reads=[kbuf, b_QT[i], b_alibi], writes=[b_pbS[sI]])
                    for c in range(2):
                        ins = nc.tensor.matmul(pbS[sI][:, c * 256 + q_lo:c * 256 + 256], lhsT=KT[c][0:68, j0:j0 + 128],
                                               rhs=QT[c][0:68, i0 + q_lo:i0 + 256], start=True, stop=True)
                    K.end(pe, ins, reads=[kbuf, b_QT[i], b_alibi], writes=[b_pbS[sI]])
                    if diag1:
                        src = pbS[sI][:].rearrange("p (c q) -> p c q", c=2)[:, :, 128:256]
                        dst = PT[pI][:].rearrange("p (c q) -> p c q", c=2)[:, :, 128:256]
                    else:
                        src = pbS[sI][:]
                        dst = PT[pI][:]
                    K.op(act, lambda: nc.scalar.activation(out=dst, in_=src, func=AF.Exp), reads=[b_pbS[sI]], writes=[b_PT[pI]])
                    if diag0 or diag1:
                        mq = 0 if diag0 else 128
                        pv = PT[pI][:].rearrange("p (c q) -> p c q", c=2)[:, :, mq:mq + 128]
                        K.op(pool, lambda: nc.gpsimd.tensor_tensor(out=pv, in0=pv, in1=trimask[:], op=ALU.mult),
                             reads=[b_masks], writes=[b_PT[pI]])
                    vbuf = b_Vb[j0 // 512]
                    K.begin(pe, reads=[b_PT[pI], vbuf], writes=[b_pbA])
                    for c in range(2):
                        for sub in range(2):
                            if diag1 and sub == 0:
                                continue
                            last = diag0 if sub == 0 else diag1
                            ins = nc.tensor.matmul(pbA[c][:, sub * 129:(sub + 1) * 129],
                                                   lhsT=PT[pI][:, c * 256 + sub * 128:c * 256 + sub * 128 + 128],
                                                   rhs=Vb[:, kt, 0:129], start=first[c], stop=last, skip_group_check=True)
                            first[c] = False
                    K.end(pe, ins, reads=[b_PT[pI], vbuf], writes=[b_pbA])
                for sub in range(2):
                    o1 = pbA[0][:, sub * 129:sub * 129 + 128]
                    l1 = pbA[0][:, sub * 129 + 128:sub * 129 + 129]
                    o2 = pbA[1][:, sub * 129:sub * 129 + 128]
                    l2 = pbA[1][:, sub * 129 + 128:sub * 129 + 129]
                    K.op(dve, lambda: nc.vector.reciprocal(out=fin[:, 0:1], in_=l1), reads=[b_pbA], writes=[b_fin])
                    K.op(dve, lambda: nc.vector.reciprocal(out=fin[:, 1:2], in_=l2), reads=[b_pbA], writes=[b_fin])
                    K.op(dve, lambda: nc.vector.tensor_tensor(out=fin[:, 2:3], in0=fin[:, 1:2], in1=cols[:, 6:7], op=ALU.mult),
                         reads=[b_cols], writes=[b_fin])
                    K.op(dve, lambda: nc.vector.tensor_scalar(out=t1[:], in0=o1, scalar1=fin[:, 0:1], scalar2=None, op0=ALU.mult),
                         reads=[b_pbA], writes=[b_t1])
                    K.op(dve, lambda: nc.vector.scalar_tensor_tensor(out=ob[:], in0=o2, scalar=fin[:, 2:3], in1=t1[:],
                                                                     op0=ALU.mult, op1=ALU.add), reads=[b_pbA, b_t1], writes=[b_ob])
                    K.op(act, lambda: nc.scalar.activation(out=junk[:, 128:256], in_=ob[:], func=AF.Square, accum_out=fin[:, 3:4]),
                         reads=[b_ob], writes=[b_fin, b_junk])
                    K.op(act, lambda: nc.scalar.activation(out=fin[:, 4:5], in_=fin[:, 3:4], func=AF.Ln, scale=1.0 / 128, bias=K.eps_col[:, 0:1]),
                         reads=[K.b_eps], writes=[b_fin])
                    K.op(act, lambda: nc.scalar.activation(out=fin[:, 5:6], in_=fin[:, 4:5], func=AF.Exp, scale=-0.5), writes=[b_fin])
                    K.op(dve, lambda: nc.vector.scalar_tensor_tensor(out=yb[:, qt * 2 + sub, :], in0=ob[:], scalar=fin[:, 5:6], in1=gn_b[:],
                                                                     op0=ALU.mult, op1=ALU.mult), reads=[b_ob, b_fin, b_gn], writes=[b_yb])
            transpose4(yst[1][i % 2][:], lambda q: yb[:, q, :], b_yb, b_yst[1][i % 2])
            K.dma(sp, s_y[1][i % 2], yT[1, :, tok0:tok0 + 512], yst[1][i % 2][:], reads=[b_yst[1][i % 2]],
                  writes=([b_y_dst] if b_y_dst else []))
        for a in range(2):
            for i in range(2):
                sp.wait({s_y[a][i]: s_y[a][i].n})
        K.barrier()


def setup_common(K):
    nc = K.nc
    K.eps_col = K.sb("eps_col", [128, 1], F32)
    K.b_eps = Buf()
    K.op(K.dve, lambda: nc.vector.memset(K.eps_col[:], EPS), writes=[K.b_eps])


def head_weight_slice(w_in_l, h):
    c = []
    c.append(w_in_l[:, 0 * 512 + h * 128:0 * 512 + (h + 1) * 128])
    c.append(w_in_l[:, 1 * 512 + h * 128:1 * 512 + (h + 1) * 128])
    qb = w_in_l[:, 2048 + h * 128:2048 + (h + 1) * 128]
    kb = w_in_l[:, 2560 + h * 128:2560 + (h + 1) * 128]
    c += [qb[:, 0:64], kb[:, 0:64], qb[:, 64:128], kb[:, 64:128]]
    c.append(w_in_l[:, 2 * 512 + h * 128:2 * 512 + (h + 1) * 128])
    c.append(w_in_l[:, 3 * 512 + h * 128:3 * 512 + (h + 1) * 128])
    c.append(w_in_l[:, 3072 + h * 128:3072 + (h + 1) * 128])
    return np.ascontiguousarray(np.concatenate(c, axis=1))


def alibi_table(h):
    slope = 2.0 ** (-8.0 / 4 * (h + 1))
    pos = np.arange(S)
    il = (pos % 128).astype(np.float32)
    ib = (pos // 128).astype(np.float32)
    one = np.ones(S, np.float32)
    t = np.stack([-slope * il, -slope * 128.0 * ib, one, one, one, one, slope * il, slope * 128.0 * ib]).astype(np.float32)
    return np.ascontiguousarray(t)


class NormT:
    def __init__(self, K, C, sb, psum, tag):
        nc = K.nc
        self.K, self.C = K, C
        self.hbf = sb("nt_hbf" + tag, [128, 1024], BF16)
        self.junk = sb("nt_junk" + tag, [128, 1024], BF16)
        self.col = sb("nt_col" + tag, [128, 4], F32)
        self.pT = psum("nt_pT" + tag, [128, 1024], BF16)
        self.b_hbf, self.b_junk, self.b_col, self.b_pT = Buf(), Buf(), Buf(), Buf()

    def run(self, xt, b_xt, gbc, b_gbc, stg, b_stg, sub):
        K, nc = self.K, self.K.nc
        K.op(K.act, lambda: nc.scalar.activation(out=self.junk[:], in_=xt, func=AF.Square, accum_out=self.col[:, 0:1]),
             reads=[b_xt], writes=[self.b_junk, self.b_col])
        K.op(K.act, lambda: nc.scalar.activation(out=self.col[:, 1:2], in_=self.col[:, 0:1], func=AF.Ln, scale=1.0 / D, bias=K.eps_col[:, 0:1]),
             reads=[K.b_eps], writes=[self.b_col])
        K.op(K.act, lambda: nc.scalar.activation(out=self.col[:, 2:3], in_=self.col[:, 1:2], func=AF.Exp, scale=-0.5), writes=[self.b_col])
        K.op(K.dve, lambda: nc.vector.scalar_tensor_tensor(out=self.hbf[:], in0=xt, scalar=self.col[:, 2:3], in1=gbc,
                                                           op0=ALU.mult, op1=ALU.mult), reads=[b_xt, self.b_col, b_gbc], writes=[self.b_hbf])
        K.begin(K.pe, reads=[self.b_hbf, self.C.b_ident], writes=[self.b_pT])
        for c in range(8):
            ins = nc.tensor.transpose(self.pT[:, c * 128:(c + 1) * 128], self.hbf[:, c * 128:(c + 1) * 128], self.C.ident[:])
        K.end(K.pe, ins, reads=[self.b_hbf, self.C.b_ident], writes=[self.b_pT])
        K.op(K.dve, lambda: nc.vector.tensor_copy(out=stg[:, :, sub * 128:(sub + 1) * 128],
                                                  in_=self.pT[:].rearrange("p (c t) -> p c t", c=8)), reads=[self.b_pT], writes=[b_stg])


def load_bc(K, sb, name, src, sem_name):
    t = sb(name, [128, D], F32)
    b = Buf()
    K.dma(K.sp, K.new_sem(sem_name), t[:], src.partition_broadcast(128), writes=[b])
    return t, b


def emit_N(K, C, x_src, gain, hT_dst, ntiles=4, tag="n"):
    nc = K.nc
    with contextlib.ExitStack() as st:
        sb = lambda n, s, d: st.enter_context(nc.sbuf_tensor(n + tag, s, d))
        psum = lambda n, s, d: st.enter_context(nc.psum_tensor(n + tag, s, d))
        gbc, b_gbc = load_bc(K, sb, "gbc", gain, "n_g" + tag)
        xt = [sb(f"xt{i}", [128, D], F32) for i in range(2)]
        b_xt = [Buf(), Buf()]
        s_xt = [K.new_sem(f"n_x{i}" + tag) for i in range(2)]
        stg = [sb(f"stg{i}", [128, 8, 512], BF16) for i in range(2)]
        b_stg = [Buf(), Buf()]
        s_st = [K.new_sem(f"n_s{i}" + tag) for i in range(2)]
        nt = NormT(K, C, sb, psum, tag)
        hv = hT_dst.rearrange("(c p) t -> p c t", p=128)
        for t in range(ntiles):
            for sub in range(4):
                k = t * 4 + sub
                K.dma(K.sp, s_xt[k % 2], xt[k % 2][:], x_src[k * 128:(k + 1) * 128, :], writes=[b_xt[k % 2]])
                nt.run(xt[k % 2][:], b_xt[k % 2], gbc[:], b_gbc, stg[t % 2], b_stg[t % 2], sub)
            K.dma(K.sp, s_st[t % 2], hv[:, :, t * 512:(t + 1) * 512], stg[t % 2][:], reads=[b_stg[t % 2]])
        for i in range(2):
            K.sp.wait({s_st[i]: s_st[i].n})
        K.barrier()


def load_w(K, sem, dst, src_view, nchunk, buf, step=1, **kw):
    for c in range(0, nchunk, step):
        K.dma(K.pool, sem, dst[:, c:c + step, :], src_view[:, c:c + step, :], **kw)
    buf.w = {sem: sem.n}


def post_norm_residual(K, sb_t, banks, b_banks, gbc, b_gbc, xres, b_xres, out_t, b_out, col, b_col, junk, b_junk):
    nc = K.nc
    for hf in range(2):
        K.op(K.act, lambda: nc.scalar.activation(out=junk[:, 0:512], in_=banks[hf][:, :], func=AF.Square, accum_out=col[:, hf:hf + 1]),
             reads=[b_banks[hf]], writes=[b_junk, b_col])
    K.op(K.dve, lambda: nc.vector.tensor_tensor(out=col[:, 2:3], in0=col[:, 0:1], in1=col[:, 1:2], op=ALU.add), reads=[b_col], writes=[b_col])
    K.op(K.act, lambda: nc.scalar.activation(out=col[:, 3:4], in_=col[:, 2:3], func=AF.Ln, scale=1.0 / D, bias=K.eps_col[:, 0:1]),
         reads=[K.b_eps, b_col], writes=[b_col])
    K.op(K.act, lambda: nc.scalar.activation(out=col[:, 4:5], in_=col[:, 3:4], func=AF.Exp, scale=-0.5), writes=[b_col])
    for hf in range(2):
        K.op(K.dve, lambda: nc.vector.scalar_tensor_tensor(out=sb_t[:, hf * 512:(hf + 1) * 512], in0=banks[hf][:, :], scalar=col[:, 4:5],
                                                           in1=gbc[:, hf * 512:(hf + 1) * 512], op0=ALU.mult, op1=ALU.mult),
             reads=[b_banks[hf], b_col, b_gbc], writes=[b_out])
    K.op(K.pool, lambda: nc.gpsimd.tensor_tensor(out=out_t, in0=sb_t[:], in1=xres, op=ALU.add), reads=[b_out, b_xres], writes=[b_out])


def emit_C1(K, C, yaT, ybT, hT_own, x_own, wg, wua, wub, wout, g_post, g_fpre, xmid_dst, h2T_dst, ntiles=4, tag="c1"):
    nc = K.nc
    pe, act, dve, pool, sp = K.pe, K.act, K.dve, K.pool, K.sp
    with contextlib.ExitStack() as st:
        sb = lambda n, s, d: st.enter_context(nc.sbuf_tensor(n + tag, s, d))
        psum = lambda n, s, d: st.enter_context(nc.psum_tensor(n + tag, s, d))
        Wg = sb("Wg", [128, 8, 2048], BF16)
        Wua = sb("Wua", [128, 4, D], BF16)
        Wub = sb("Wub", [128, 4, D], BF16)
        Wo = sb("Wo", [128, 8, D], BF16)
        b_Wg, b_Wu, b_Wo = Buf(), Buf(), Buf()
        load_w(K, K.new_sem("c1wg"), Wg, wg.rearrange("(c p) n -> p c n", p=128), 8, b_Wg, max_dma_last_dim=4096)
        s_wu = K.new_sem("c1wu")
        load_w(K, s_wu, Wua, wua.rearrange("(c p) n -> p c n", p=128), 4, b_Wu)
        load_w(K, s_wu, Wub, wub.rearrange("(c p) n -> p c n", p=128), 4, b_Wu)
        load_w(K, K.new_sem("c1wo"), Wo, wout.rearrange("(c p) n -> p c n", p=128), 8, b_Wo)
        gpost, b_gpost = load_bc(K, sb, "gpost", g_post, "c1g1")
        gfpre, b_gfpre = load_bc(K, sb, "gfpre", g_fpre, "c1g2")
        hTt = sb("hTt", [128, 8, 512], BF16)
        yat = sb("yat", [128, 4, 512], BF16)
        ybt = sb("ybt", [128, 4, 512], BF16)
        b_hTt, b_yat, b_ybt = Buf(), Buf(), Buf()
        s_in = [K.new_sem(f"c1in{i}") for i in range(3)]
        mT = sb("mT", [128, 8, 512], BF16)
        b_mT = Buf()
        sga = sb("sga", [128, 512], F32)
        sgb = sb("sgb", [128, 512], F32)
        ta = sb("ta", [128, 512], F32)
        tb = sb("tb", [128, 512], F32)
        b_sga, b_sgb, b_ta, b_tb = Buf(), Buf(), Buf(), Buf()
        P = [psum(f"P{i}", [128, 512], F32) for i in range(4)]
        b_P = [Buf() for _ in range(4)]
        xt = sb("xt", [128, D], F32)
        b_xt = Buf()
        s_x = K.new_sem("c1x")
        tmp = sb("tmp", [128, D], F32)
        b_tmp = Buf()
        col = sb("col", [128, 8], F32)
        b_col = Buf()
        junk = sb("junk", [128, 512], F32)
        b_junk = Buf()
        stg = sb("stg", [128, 8, 512], BF16)
        b_stg = Buf()
        s_o = [K.new_sem(f"c1o{i}") for i in range(2)]
        nt = NormT(K, C, sb, psum, tag)
        hv = hT_own.rearrange("(c p) t -> p c t", p=128)
        yav = yaT.rearrange("(c p) t -> p c t", p=128)
        ybv = ybT.rearrange("(c p) t -> p c t", p=128)
        h2v = h2T_dst.rearrange("(c p) t -> p c t", p=128)
        for t in range(ntiles):
            ts_ = slice(t * 512, (t + 1) * 512)
            K.dma(sp, s_in[0], hTt[:], hv[:, :, ts_], writes=[b_hTt])
            K.dma(sp, s_in[1], yat[:], yav[:, :, ts_], writes=[b_yat])
            K.dma(sp, s_in[2], ybt[:], ybv[:, :, ts_], writes=[b_ybt])
            for m in range(8):
                ms = slice(m * 128, (m + 1) * 128)

                def grp(bank, n, lhs_fn, rhs_fn, reads):
                    K.begin(pe, reads=reads, writes=[b_P[bank]])
                    for c in range(n):
                        ins = nc.tensor.matmul(P[bank][:, :], lhsT=lhs_fn(c), rhs=rhs_fn(c), start=(c == 0), stop=(c == n - 1))
                    K.end(pe, ins, reads=reads, writes=[b_P[bank]])

                grp(0, 8, lambda c: Wg[:, c, ms], lambda c: hTt[:, c, :], [b_Wg, b_hTt])
                grp(1, 4, lambda c: Wua[:, c, ms], lambda c: yat[:, c, :], [b_Wu, b_yat])
                grp(2, 8, lambda c: Wg[:, c, 1024 + m * 128:1024 + (m + 1) * 128], lambda c: hTt[:, c, :], [b_Wg, b_hTt])
                grp(3, 4, lambda c: Wub[:, c, ms], lambda c: ybt[:, c, :], [b_Wu, b_ybt])
                K.op(act, lambda: nc.scalar.activation(out=sga[:], in_=P[0][:, :], func=AF.Sigmoid), reads=[b_P[0]], writes=[b_sga])
                K.op(act, lambda: nc.scalar.activation(out=sgb[:], in_=P[2][:, :], func=AF.Sigmoid), reads=[b_P[2]], writes=[b_sgb])
                K.op(dve, lambda: nc.vector.tensor_tensor(out=ta[:], in0=P[1][:, :], in1=sga[:], op=ALU.mult), reads=[b_P[1], b_sga], writes=[b_ta])
                K.op(dve, lambda: nc.vector.tensor_tensor(out=tb[:], in0=P[3][:, :], in1=sgb[:], op=ALU.mult), reads=[b_P[3], b_sgb], writes=[b_tb])
                K.op(pool, lambda: nc.gpsimd.tensor_tensor(out=mT[:, m, :], in0=ta[:], in1=tb[:], op=ALU.add), reads=[b_ta, b_tb], writes=[b_mT])
            for sub in range(4):
                ss_ = slice(sub * 128, (sub + 1) * 128)
                k = t * 4 + sub
                K.dma(sp, s_x, xt[:], x_own[k * 128:(k + 1) * 128, :], writes=[b_xt])
                for hf in range(2):
                    K.begin(pe, reads=[b_mT, b_Wo], writes=[b_P[hf]])
                    for c in range(8):
                        ins = nc.tensor.matmul(P[hf][:, :], lhsT=mT[:, c, ss_], rhs=Wo[:, c, hf * 512:(hf + 1) * 512], start=(c == 0), stop=(c == 7))
                    K.end(pe, ins, reads=[b_mT, b_Wo], writes=[b_P[hf]])
                post_norm_residual(K, tmp, P[0:2], b_P[0:2], gpost, b_gpost, xt[:], b_xt, tmp[:], b_tmp, col, b_col, junk, b_junk)
                K.dma(sp, s_o[0], xmid_dst[k * 128:(k + 1) * 128, :], tmp[:], reads=[b_tmp])
                nt.run(tmp[:], b_tmp, gfpre[:], b_gfpre, stg, b_stg, sub)
            K.dma(sp, s_o[1], h2v[:, :, ts_], stg[:], reads=[b_stg])
        for i in range(2):
            sp.wait({s_o[i]: s_o[i].n})
        K.barrier()


def emit_C2(K, C, h2T, xmid, wfg, wfu, wfd, g_fpost, x_dst, g_next=None, hTn_dst=None, ntiles=4, tag="c2"):
    nc = K.nc
    pe, act, dve, pool, sp = K.pe, K.act, K.dve, K.pool, K.sp
    with contextlib.ExitStack() as st:
        sb = lambda n, s, d: st.enter_context(nc.sbuf_tensor(n + tag, s, d))
        psum = lambda n, s, d: st.enter_context(nc.psum_tensor(n + tag, s, d))
        Wfg = sb("Wfg", [128, 8, DFF], BF16)
        Wfu = sb("Wfu", [128, 8, DFF], BF16)
        Wfd = sb("Wfd", [128, NFC, D], BF16)
        b_Wfg, b_Wfu, b_Wfd = Buf(), Buf(), Buf()
        load_w(K, K.new_sem("c2wg"), Wfg, wfg.rearrange("(c p) n -> p c n", p=128), 8, b_Wfg, max_dma_last_dim=4096)
        load_w(K, K.new_sem("c2wu"), Wfu, wfu.rearrange("(c p) n -> p c n", p=128), 8, b_Wfu, max_dma_last_dim=4096)
        load_w(K, K.new_sem("c2wd"), Wfd, wfd.rearrange("(c p) n -> p c n", p=128), NFC, b_Wfd, step=2)
        gfpost, b_gfpost = load_bc(K, sb, "gfpost", g_fpost, "c2g1")
        if g_next is not None:
            gnext, b_gnext = load_bc(K, sb, "gnext", g_next, "c2g2")
            stg = sb("stg", [128, 8, 512], BF16)
            b_stg = Buf()
            nt = NormT(K, C, sb, psum, tag)
            hnv = hTn_dst.rearrange("(c p) t -> p c t", p=128)
        h2t = sb("h2t", [128, 8, 512], BF16)
        b_h2t = Buf()
        s_in = K.new_sem("c2in")
        ffT = sb("ffT", [128, NFC, 512], BF16)
        b_ffT = Buf()
        sgm = sb("sgm", [128, 512], F32)
        tt = sb("tt", [128, 512], F32)
        b_sgm, b_tt = Buf(), Buf()
        P = [psum(f"P{i}", [128, 512], F32) for i in range(4)]
        b_P = [Buf() for _ in range(4)]
        xt = sb("xt", [128, D], F32)
        b_xt = Buf()
        s_x = K.new_sem("c2x")
        tmp = sb("tmp", [128, D], F32)
        b_tmp = Buf()
        col = sb("col", [128, 8], F32)
        b_col = Buf()
        junk = sb("junk", [128, 512], F32)
        b_junk = Buf()
        s_o = [K.new_sem(f"c2o{i}") for i in range(2)]
        h2v = h2T.rearrange("(c p) t -> p c t", p=128)
        for t in range(ntiles):
            ts_ = slice(t * 512, (t + 1) * 512)
            K.dma(sp, s_in, h2t[:], h2v[:, :, ts_], writes=[b_h2t])
            for f in range(NFC):
                fs = slice(f * 128, (f + 1) * 128)
                pg, pu = 2 * (f % 2), 2 * (f % 2) + 1
                for bank, Wt, bW in ((pg, Wfg, b_Wfg), (pu, Wfu, b_Wfu)):
                    K.begin(pe, reads=[bW, b_h2t], writes=[b_P[bank]])
                    for c in range(8):
                        ins = nc.tensor.matmul(P[bank][:, :], lhsT=Wt[:, c, fs], rhs=h2t[:, c, :], start=(c == 0), stop=(c == 7))
                    K.end(pe, ins, reads=[bW, b_h2t], writes=[b_P[bank]])
                K.op(act, lambda: nc.scalar.activation(out=sgm[:], in_=P[pg][:, :], func=AF.Sigmoid), reads=[b_P[pg]], writes=[b_sgm])
                K.op(dve, lambda: nc.vector.tensor_tensor(out=tt[:], in0=P[pg][:, :], in1=sgm[:], op=ALU.mult), reads=[b_P[pg], b_sgm], writes=[b_tt])
                K.op(dve, lambda: nc.vector.tensor_tensor(out=ffT[:, f, :], in0=P[pu][:, :], in1=tt[:], op=ALU.mult), reads=[b_P[pu], b_tt], writes=[b_ffT])
            for sub in range(4):
                ss_ = slice(sub * 128, (sub + 1) * 128)
                k = t * 4 + sub
                K.dma(sp, s_x, xt[:], xmid[k * 128:(k + 1) * 128, :], writes=[b_xt])
                for hf in range(2):
                    K.begin(pe, reads=[b_ffT, b_Wfd], writes=[b_P[hf]])
                    for f in range(NFC):
                        ins = nc.tensor.matmul(P[hf][:, :], lhsT=ffT[:, f, ss_], rhs=Wfd[:, f, hf * 512:(hf + 1) * 512], start=(f == 0), stop=(f == NFC - 1))
                    K.end(pe, ins, reads=[b_ffT, b_Wfd], writes=[b_P[hf]])
                post_norm_residual(K, tmp, P[0:2], b_P[0:2], gfpost, b_gfpost, xt[:], b_xt, tmp[:], b_tmp, col, b_col, junk, b_junk)
                K.dma(sp, s_o[0], x_dst[k * 128:(k + 1) * 128, :], tmp[:], reads=[b_tmp])
                if g_next is not None:
                    nt.run(tmp[:], b_tmp, gnext[:], b_gnext, stg, b_stg, sub)
            if g_next is not None:
                K.dma(sp, s_o[1], hnv[:, :, ts_], stg[:], reads=[b_stg])
        for i in range(2):
            if s_o[i].n:
                sp.wait({s_o[i]: s_o[i].n})
        K.barrier()


def _new_nc():
    return bass.Bass("TRN2", target_bir_lowering=False)


def _din(nc, name, shape, dt):
    return nc.dram_tensor(name, list(shape), dt, kind="ExternalInput").ap()


def _dout(nc, name, shape, dt):
    return nc.dram_tensor(name, list(shape), dt, kind="ExternalOutput").ap()


def build_N(ntiles=4):
    nc = _new_nc()
    x = _din(nc, "x_own", [TS, D], F32)
    g = _din(nc, "gain", [1, D], F32)
    hT = _dout(nc, "hT_own", [D, TS], BF16)
    with contextlib.ExitStack() as st:
        K = Kctx(nc, st)
        setup_common(K)
        C = make_consts(K)
        emit_N(K, C, x, g, hT, ntiles=ntiles)
    return nc


def build_AB(l, ntiles=NT):
    nc = _new_nc()
    hT = _din(nc, "hT", [D, S], BF16)
    w_hd = _din(nc, "w_hd", [D, 896], F32)
    lbh = _din(nc, "lbh", [128, 2], F32)
    rowv = _din(nc, "rowv", [1, 512], F32)
    alibi = _din(nc, "alibi", [8, S], F32)
    yT = _dout(nc, "yT", [2, 128, S], BF16)
    with contextlib.ExitStack() as st:
        K = Kctx(nc, st)
        setup_common(K)
        C = make_consts(K)
        hv = hT.rearrange("(c p) t -> p c t", p=128)
        emit_AB(K, C, l, lambda i: hv[:, :, i * 512:(i + 1) * 512], w_hd, lbh, rowv, alibi, yT, ntiles=ntiles)
    return nc


def build_C1(ntiles=4):
    nc = _new_nc()
    yaT = _din(nc, "yaT", [512, TS], BF16)
    ybT = _din(nc, "ybT", [512, TS], BF16)
    hT = _din(nc, "hT_own", [D, TS], BF16)
    x = _din(nc, "x_own", [TS, D], F32)
    wg = _din(nc, "wg", [D, 2048], F32)
    wua = _din(nc, "wua", [512, D], F32)
    wub = _din(nc, "wub", [512, D], F32)
    wo = _din(nc, "wout", [D, D], F32)
    g1 = _din(nc, "g_post", [1, D], F32)
    g2 = _din(nc, "g_fpre", [1, D], F32)
    xm = _dout(nc, "xmid", [TS, D], F32)
    h2 = _dout(nc, "h2T", [D, TS], BF16)
    with contextlib.ExitStack() as st:
        K = Kctx(nc, st)
        setup_common(K)
        C = make_consts(K)
        emit_C1(K, C, yaT, ybT, hT, x, wg, wua, wub, wo, g1, g2, xm, h2, ntiles=ntiles)
    return nc


def build_C2(last, ntiles=4):
    nc = _new_nc()
    h2 = _din(nc, "h2T", [D, TS], BF16)
    xm = _din(nc, "xmid", [TS, D], F32)
    wfg = _din(nc, "wfg", [D, DFF], F32)
    wfu = _din(nc, "wfu", [D, DFF], F32)
    wfd = _din(nc, "wfd", [DFF, D], F32)
    g1 = _din(nc, "g_fpost", [1, D], F32)
    xo = _dout(nc, "x_new", [TS, D], F32)
    gn = hn = None
    if not last:
        gn = _din(nc, "g_next", [1, D], F32)
        hn = _dout(nc, "hT_next", [D, TS], BF16)
    with contextlib.ExitStack() as st:
        K = Kctx(nc, st)
        setup_common(K)
        C = make_consts(K)
        emit_C2(K, C, h2, xm, wfg, wfu, wfd, g1, xo, gn, hn, ntiles=ntiles)
    return nc


def _run(nc, in_maps):
    res = run_bass_kernel_spmd(nc, in_maps, core_ids=list(range(8)))
    return res.results


def kernel(x, lower_bounds, norm_mix_pre, norm_mix_post, norm_ffn_pre, norm_ffn_post, w_in, hg_out_norm, da_subln,
           lambda_q1, lambda_k1, lambda_q2, lambda_k2, w_up_a, w_up_b, w_out, w_ffn_gate, w_ffn_up, w_ffn_down):
    f = lambda a: np.ascontiguousarray(np.asarray(a, dtype=np.float32))
    x = f(x)
    cores = [(c // 4, c % 4) for c in range(8)]
    x_own = [np.ascontiguousarray(x[b, j * TS:(j + 1) * TS]) for b, j in cores]
    r = _run(build_N(), [{"x_own": x_own[c], "gain": f(norm_mix_pre[0])[None]} for c in range(8)])
    hT_own = [np.asarray(r[c]["hT_own"]) for c in range(8)]
    for l in range(DEPTH):
        hT_b = [np.ascontiguousarray(np.concatenate([hT_own[b * 4 + j] for j in range(4)], axis=1)) for b in range(NB)]
        rowv = np.concatenate([f(hg_out_norm[l]), f(da_subln[l]), f(lambda_q1[l]), f(lambda_k1[l]), f(lambda_q2[l]), f(lambda_k2[l])])[None]
        w_in_l = f(w_in[l])
        lbs = f(lower_bounds)
        r = _run(build_AB(l), [{"hT": hT_b[b], "w_hd": head_weight_slice(w_in_l, j),
                                "lbh": np.ascontiguousarray(lbs[:, j * 128:(j + 1) * 128].T),
                                "rowv": np.ascontiguousarray(rowv), "alibi": alibi_table(j)} for b, j in cores])
        yT = [np.asarray(r[c]["yT"]) for c in range(8)]
        wg = np.ascontiguousarray(w_in_l[:, 3584:5632])
        maps = []
        for c, (b, j) in enumerate(cores):
            ya = np.ascontiguousarray(np.concatenate([yT[b * 4 + h][0][:, j * TS:(j + 1) * TS] for h in range(4)], axis=0))
            yb = np.ascontiguousarray(np.concatenate([yT[b * 4 + h][1][:, j * TS:(j + 1) * TS] for h in range(4)], axis=0))
            maps.append({"yaT": ya, "ybT": yb, "hT_own": hT_own[c], "x_own": x_own[c], "wg": wg, "wua": f(w_up_a[l]), "wub": f(w_up_b[l]),
                         "wout": f(w_out[l]), "g_post": f(norm_mix_post[l])[None], "g_fpre": f(norm_ffn_pre[l])[None]})
        r = _run(build_C1(), maps)
        last = (l == DEPTH - 1)
        maps = []
        for c in range(8):
            m = {"h2T": np.asarray(r[c]["h2T"]), "xmid": np.asarray(r[c]["xmid"]), "wfg": f(w_ffn_gate[l]), "wfu": f(w_ffn_up[l]),
                 "wfd": f(w_ffn_down[l]), "g_fpost": f(norm_ffn_post[l])[None]}
            if not last:
                m["g_next"] = f(norm_mix_pre[l + 1])[None]
            maps.append(m)
        r = _run(build_C2(last), maps)
        x_own = [np.asarray(r[c]["x_new"]) for c in range(8)]
        if not last:
            hT_own = [np.asarray(r[c]["hT_next"]) for c in range(8)]
    out = np.zeros((NB, S, D), np.float32)
    for c, (b, j) in enumerate(cores):
        out[b, j * TS:(j + 1) * TS] = x_own[c]
    return out
```
